# Optimizing a Trainium2 kernel written in Bass

```python
import jax, jax.numpy as jnp
from jax import lax
import numpy as np

D_MODEL = 1024
BATCH = 4
SEQ = 8192
DEPTH = 4

GRID_W = 64
CTX_LEN = 256
N_MIXERS = 2
N_A_LAYERS = (DEPTH + 1) // 2
N_B_LAYERS = DEPTH // 2
N_VRES = max(N_B_LAYERS - 1, 0)
D_FF = 2816
HGRN_HEADS = 8
HGRN_DK = D_MODEL // HGRN_HEADS
HGRN_CHUNK = 64
RWKV_HEAD = 64
RWKV_HEADS = D_MODEL // RWKV_HEAD
RWKV_DECAY_LORA = 64
RWKV_AAA_LORA = 64
RWKV_MV_LORA = 32
RWKV_GATE_LORA = 128
RMS_EPS = 1e-6
GN_EPS = 64e-5

kernel_name = 'hgrn2_rwkv7_macaron_prefix_trunk'


def rms_norm(x, g):
    xf = x.astype(jnp.float32)
    y = xf * lax.rsqrt(jnp.mean(xf * xf, axis=-1, keepdims=True) + RMS_EPS)
    return (y * g.astype(jnp.float32)).astype(x.dtype)


def modulate(x, g, shift, scale):
    return rms_norm(x, g) * (1 + scale) + shift


def mod_part(m, sub, k):
    start = (3 * sub + k) * D_MODEL
    return m[..., start:start + D_MODEL]


def swiglu(x, w_in, w_out):
    gate, up = jnp.split(x @ w_in, 2, axis=-1)
    return (jax.nn.silu(gate) * up) @ w_out


def to_heads(a, n_heads):
    b_, t_, d_ = a.shape
    return a.reshape(b_, t_, n_heads, d_ // n_heads).transpose(0, 2, 1, 3)


def from_heads(a):
    b_, h_, t_, dh = a.shape
    return a.transpose(0, 2, 1, 3).reshape(b_, t_, h_ * dh)


def head_rms(o):
    of = o.astype(jnp.float32)
    return from_heads(of * lax.rsqrt(jnp.mean(of * of, axis=-1, keepdims=True) + RMS_EPS))


def hgrn_gates(z, lb):
    zf = z.astype(jnp.float32)
    key = (1 - lb) * jax.nn.sigmoid(-zf)
    logf = jnp.log(lb + (1 - lb) * jax.nn.sigmoid(zf))
    return to_heads(key, HGRN_HEADS), to_heads(logf, HGRN_HEADS)


def gla_chunk_scan(q, k, v, logf, s0):
    b_, h_, t_, _ = k.shape
    dv = v.shape[-1]
    n = t_ // HGRN_CHUNK

    def chunks(a):
        a = a.astype(jnp.float32).reshape(b_, h_, n, HGRN_CHUNK, a.shape[-1])
        return jnp.moveaxis(a, 2, 0)

    with_out = q is not None
    mask = jnp.tril(jnp.ones((HGRN_CHUNK, HGRN_CHUNK), bool))[:, :, None]
    xs = (chunks(k), chunks(v), chunks(logf)) + ((chunks(q),) if with_out else ())

    def step(s, inp):
        kc, vc, gc = inp[:3]
        bcum = jnp.cumsum(gc, axis=2)
        btot = bcum[:, :, -1:, :]
        s_new = (jnp.exp(btot)[:, :, 0, :, None] * s
                 + jnp.einsum('bhck,bhcv->bhkv', kc * jnp.exp(btot - bcum), vc))
        if not with_out:
            return s_new, None
        qc = inp[3]
        o_inter = jnp.einsum('bhck,bhkv->bhcv', qc * jnp.exp(bcum), s)
        diff = bcum[:, :, :, None, :] - bcum[:, :, None, :, :]
        dec = jnp.where(mask, jnp.exp(jnp.where(mask, diff, 0.0)), 0.0)
        att = jnp.einsum('bhik,bhijk,bhjk->bhij', qc, dec, kc)
        return s_new, o_inter + jnp.einsum('bhij,bhjv->bhiv', att, vc)

    s_fin, o = lax.scan(step, s0, xs)
    if with_out:
        o = jnp.moveaxis(o, 0, 2).reshape(b_, h_, t_, dv)
    return o, s_fin


def gla_dir(q, k, v, logf, s0, reverse):
    if reverse:
        flip = lambda a: None if a is None else jnp.flip(a, axis=2)
        o, s = gla_chunk_scan(flip(q), flip(k), flip(v), flip(logf), s0)
        return flip(o), s
    return gla_chunk_scan(q, k, v, logf, s0)


def hgrn2_mixer(u, uc, w_in, lb, norm_g, w_out, ctx_out):
    d = D_MODEL
    p_lat = u @ w_in
    p_ctx = uc @ (w_in if ctx_out else w_in[:, :3 * d])

    def state_side(p):
        v = to_heads(p[..., :d].astype(jnp.float32), HGRN_HEADS)
        gates = [hgrn_gates(p[..., (1 + n) * d:(2 + n) * d], lb[n]) for n in range(2)]
        return v, gates

    def out_side(p):
        return to_heads(jax.nn.silu(p[..., 3 * d:4 * d]), HGRN_HEADS), p[..., 4 * d:]

    v_l, gates_l = state_side(p_lat)
    v_c, gates_c = state_side(p_ctx)
    q_l, g_l = out_side(p_lat)
    q_c, g_c = out_side(p_ctx) if ctx_out else (None, None)
    s0 = jnp.zeros((u.shape[0], HGRN_HEADS, HGRN_DK, HGRN_DK), jnp.float32)
    o_l = 0.0
    o_c = 0.0
    for n, reverse in enumerate((False, True)):
        (k_c, f_c), (k_l, f_l) = gates_c[n], gates_l[n]
        oc_n, s_ctx = gla_dir(q_c, k_c, v_c, f_c, s0, reverse)
        ol_n, _ = gla_dir(q_l, k_l, v_l, f_l, s_ctx, reverse)
        o_l = o_l + ol_n
        if ctx_out:
            o_c = o_c + oc_n

    def readout(o, g, dtype):
        return ((head_rms(o) * norm_g).astype(dtype) * jax.nn.silu(g)) @ w_out

    y = readout(o_l, g_l, u.dtype)
    yc = readout(o_c, g_c, uc.dtype) if ctx_out else None
    return y, yc


def token_shift_grid(x):
    b_, t_, d_ = x.shape
    rows = t_ // GRID_W
    xq = x.reshape(b_, rows, GRID_W, 4, d_ // 4)
    left = jnp.pad(xq[:, :, :-1, 0], ((0, 0), (0, 0), (1, 0), (0, 0)))
    right = jnp.pad(xq[:, :, 1:, 1], ((0, 0), (0, 0), (0, 1), (0, 0)))
    up = jnp.pad(xq[:, :-1, :, 2], ((0, 0), (1, 0), (0, 0), (0, 0)))
    down = jnp.pad(xq[:, 1:, :, 3], ((0, 0), (0, 1), (0, 0), (0, 0)))
    return jnp.stack([left, right, up, down], axis=3).reshape(b_, t_, d_)


def token_shift_seq(x):
    b_, t_, d_ = x.shape
    xh = x.reshape(b_, t_, 2, d_ // 2)
    prev = jnp.pad(xh[:, :-1, 0], ((0, 0), (1, 0), (0, 0)))
    nxt = jnp.pad(xh[:, 1:, 1], ((0, 0), (0, 1), (0, 0)))
    return jnp.stack([prev, nxt], axis=2).reshape(b_, t_, d_)


def l2norm_heads(a):
    b_, t_, d_ = a.shape
    ah = a.astype(jnp.float32).reshape(b_, t_, RWKV_HEADS, RWKV_HEAD)
    ah = ah * lax.rsqrt(jnp.maximum(jnp.sum(ah * ah, axis=-1, keepdims=True), 1e-24))
    return ah.reshape(b_, t_, d_)


def rwkv_features(x, xs, lp, vres, v_first, need_out):
    mu, w_rkv, w0, w1, w2, a0, a1, a2, g1, g2, k_k, k_a = lp
    xx = xs - x
    xw, xk, xv, xa = (x + xx * mu[n] for n in (1, 2, 3, 4))
    k = xk @ w_rkv[1]
    v = xv @ w_rkv[2]
    if vres is None:
        v_first = v
    else:
        v0, v1, v2 = vres
        v = v + (v_first - v) * jax.nn.sigmoid(v0 + (xv @ v1) @ v2)
    kk = l2norm_heads(k * k_k)
    dirs = []
    for n in range(2):
        wpre = (w0[n] + jnp.tanh(xw @ w1[n]) @ w2[n]).astype(jnp.float32)
        decay = jnp.exp(-jnp.exp(-jax.nn.softplus(-wpre) - 0.5))
        a = jax.nn.sigmoid((a0[n] + (xa @ a1[n]) @ a2[n]).astype(jnp.float32))
        k_n = k.astype(jnp.float32) * (1 + (a - 1) * k_a)
        dirs.append((decay, k_n, kk * a))
    r = g = None
    if need_out:
        r = (x + xx * mu[0]) @ w_rkv[0]
        g = jax.nn.sigmoid((x + xx * mu[5]) @ g1) @ g2
    return r, g, v, kk, dirs, v_first


def rwkv_scan(decay, k, v, kk, b, r, s0, reverse):
    b_, t_, d_ = k.shape

    def tm(a):
        return jnp.moveaxis(a.astype(jnp.float32).reshape(b_, t_, RWKV_HEADS, RWKV_HEAD), 1, 0)

    with_out = r is not None
    xs = (tm(decay), tm(k), tm(v), tm(kk), tm(b)) + ((tm(r),) if with_out else ())

    def step(s, inp):
        dec, kt, vt, kkt, bt = inp[:5]
        sa = jnp.einsum('bhvk,bhk->bhv', s, kkt)
        s = s * dec[:, :, None, :] - sa[..., None] * bt[:, :, None, :] + vt[..., None] * kt[:, :, None, :]
        y = jnp.einsum('bhvk,bhk->bhv', s, inp[5]) if with_out else None
        return s, y

    s_fin, y = lax.scan(step, s0, xs, reverse=reverse)
    if with_out:
        y = jnp.moveaxis(y, 0, 1).reshape(b_, t_, d_)
    return y, s_fin


def rwkv_readout(y, r, k_f, k_b, v, g, r_k, lnx_w, lnx_b, w_o, dtype):
    b_, t_, d_ = y.shape
    shp = (b_, t_, RWKV_HEADS, RWKV_HEAD)
    yh = y.reshape(shp)
    mean = jnp.mean(yh, axis=-1, keepdims=True)
    var = jnp.mean(jnp.square(yh - mean), axis=-1, keepdims=True)
    yn = ((yh - mean) * lax.rsqrt(var + GN_EPS)).reshape(b_, t_, d_) * lnx_w + lnx_b
    kb = (0.5 * (k_f + k_b)).reshape(shp)
    bonus = jnp.sum(r.astype(jnp.float32).reshape(shp) * kb * r_k, axis=-1, keepdims=True) * v.astype(jnp.float32).reshape(shp)
    return ((yn + bonus.reshape(b_, t_, d_)).astype(dtype) * g) @ w_o


def rwkv7_mixer(u, uc, lp, r_k, lnx_w, lnx_b, w_o, vres, v_first, v_first_c, ctx_out):
    r, g, v, kk, dirs, v_first = rwkv_features(u, token_shift_grid(u), lp, vres, v_first, True)
    rc, gc, vc, kkc, dirs_c, v_first_c = rwkv_features(uc, token_shift_seq(uc), lp, vres, v_first_c, ctx_out)
    s0 = jnp.zeros((u.shape[0], RWKV_HEADS, RWKV_HEAD, RWKV_HEAD), jnp.float32)
    y = 0.0
    yc = 0.0
    for n, reverse in enumerate((False, True)):
        dec_c, k_c, b_c = dirs_c[n]
        yc_n, s_ctx = rwkv_scan(dec_c, k_c, vc, kkc, b_c, rc, s0, reverse)
        dec_l, k_l, b_l = dirs[n]
        y_n, _ = rwkv_scan(dec_l, k_l, v, kk, b_l, r, s_ctx, reverse)
        y = y + y_n
        if ctx_out:
            yc = yc + yc_n
    out = rwkv_readout(y, r, dirs[0][1], dirs[1][1], v, g, r_k, lnx_w, lnx_b, w_o, u.dtype)
    out_c = rwkv_readout(yc, rc, dirs_c[0][1], dirs_c[1][1], vc, gc, r_k, lnx_w, lnx_b, w_o, uc.dtype) if ctx_out else None
    return out, out_c, v_first, v_first_c


def setup_inputs(seed: int = 0) -> dict:
    key = jax.random.key(seed)
    ks = iter(jax.random.split(key, 48))
    d = D_MODEL
    na, nb, nv = N_A_LAYERS, N_B_LAYERS, N_VRES

    def nrm(shape, s):
        return jax.random.normal(next(ks), shape, jnp.float32) * s

    return {
        'x': nrm((BATCH, SEQ, d), 1.0),
        'c': nrm((BATCH, d), 1.0),
        'ctx': nrm((BATCH, CTX_LEN, d), 1.0),
        'c_ctx': nrm((d,), 1.0),
        'norm_w': 1.0 + nrm((DEPTH, 3, d), 0.02),
        'mod_w': nrm((DEPTH, d, 9 * d), 0.5 * d ** -0.5),
        'mod_b': nrm((DEPTH, 9 * d), 0.02),
        'ffn_w_in': nrm((DEPTH, 2, d, 2 * D_FF), d ** -0.5),
        'ffn_w_out': nrm((DEPTH, 2, D_FF, d), D_FF ** -0.5),
        'hgrn_w_in': nrm((na, d, 5 * d), d ** -0.5),
        'hgrn_lb': nrm((na, 2, d), 0.5),
        'hgrn_norm_w': 1.0 + nrm((na, d), 0.02),
        'hgrn_w_out': nrm((na, d, d), d ** -0.5),
        'rwkv_mu': jax.random.uniform(next(ks), (nb, 6, d), jnp.float32),
        'rwkv_w_rkv': nrm((nb, 3, d, d), d ** -0.5),
        'rwkv_w_o': nrm((nb, d, d), d ** -0.5),
        'rwkv_w0': -3.0 + nrm((nb, 2, d), 1.5),
        'rwkv_w1': nrm((nb, 2, d, RWKV_DECAY_LORA), d ** -0.5),
        'rwkv_w2': nrm((nb, 2, RWKV_DECAY_LORA, d), 0.5 * RWKV_DECAY_LORA ** -0.5),
        'rwkv_a0': nrm((nb, 2, d), 0.5),
        'rwkv_a1': nrm((nb, 2, d, RWKV_AAA_LORA), d ** -0.5),
        'rwkv_a2': nrm((nb, 2, RWKV_AAA_LORA, d), 0.5 * RWKV_AAA_LORA ** -0.5),
        'rwkv_v0': nrm((nv, d), 0.5),
        'rwkv_v1': nrm((nv, d, RWKV_MV_LORA), d ** -0.5),
        'rwkv_v2': nrm((nv, RWKV_MV_LORA, d), 0.5 * RWKV_MV_LORA ** -0.5),
        'rwkv_g1': nrm((nb, d, RWKV_GATE_LORA), d ** -0.5),
        'rwkv_g2': nrm((nb, RWKV_GATE_LORA, d), RWKV_GATE_LORA ** -0.5),
        'rwkv_k_k': 1.0 + nrm((nb, d), 0.1),
        'rwkv_k_a': 1.0 + nrm((nb, d), 0.1),
        'rwkv_r_k': nrm((nb, RWKV_HEADS, RWKV_HEAD), 0.1),
        'rwkv_lnx_w': 1.0 + nrm((nb, d), 0.02),
        'rwkv_lnx_b': nrm((nb, d), 0.02),
        'final_norm_w': 1.0 + nrm((d,), 0.02),
    }


def reference(x, c, ctx, c_ctx, norm_w, mod_w, mod_b, ffn_w_in, ffn_w_out,
              hgrn_w_in, hgrn_lb, hgrn_norm_w, hgrn_w_out,
              rwkv_mu, rwkv_w_rkv, rwkv_w_o, rwkv_w0, rwkv_w1, rwkv_w2,
              rwkv_a0, rwkv_a1, rwkv_a2, rwkv_v0, rwkv_v1, rwkv_v2,
              rwkv_g1, rwkv_g2, rwkv_k_k, rwkv_k_a, rwkv_r_k, rwkv_lnx_w, rwkv_lnx_b,
              final_norm_w):
    p = jax.nn.softmax(hgrn_lb.astype(jnp.float32), axis=0)
    lower_bounds = jnp.cumsum(p, axis=0) - p[0]
    s_lat = jax.nn.silu(c)[:, None, :]
    s_ctx = jax.nn.silu(c_ctx)
    h, hc = x, ctx
    v_first = v_first_c = None
    for i in range(DEPTH):
        last = i == DEPTH - 1
        j = i // N_MIXERS
        m = s_lat @ mod_w[i] + mod_b[i]
        n_cols = (5 if last else 9) * D_MODEL
        mc = s_ctx @ mod_w[i][:, :n_cols] + mod_b[i][:n_cols]
        h = h + 0.5 * mod_part(m, 0, 2) * swiglu(modulate(h, norm_w[i, 0], mod_part(m, 0, 0), mod_part(m, 0, 1)), ffn_w_in[i, 0], ffn_w_out[i, 0])
        hc = hc + 0.5 * mod_part(mc, 0, 2) * swiglu(modulate(hc, norm_w[i, 0], mod_part(mc, 0, 0), mod_part(mc, 0, 1)), ffn_w_in[i, 0], ffn_w_out[i, 0])
        u = modulate(h, norm_w[i, 1], mod_part(m, 1, 0), mod_part(m, 1, 1))
        uc = modulate(hc, norm_w[i, 1], mod_part(mc, 1, 0), mod_part(mc, 1, 1))
        if i % N_MIXERS == 0:
            y, yc = hgrn2_mixer(u, uc, hgrn_w_in[j], lower_bounds[j], hgrn_norm_w[j], hgrn_w_out[j], not last)
        else:
            lp = (rwkv_mu[j], rwkv_w_rkv[j], rwkv_w0[j], rwkv_w1[j], rwkv_w2[j], rwkv_a0[j], rwkv_a1[j], rwkv_a2[j],
                  rwkv_g1[j], rwkv_g2[j], rwkv_k_k[j], rwkv_k_a[j])
            vres = None if j == 0 else (rwkv_v0[j - 1], rwkv_v1[j - 1], rwkv_v2[j - 1])
            y, yc, v_first, v_first_c = rwkv7_mixer(u, uc, lp, rwkv_r_k[j], rwkv_lnx_w[j], rwkv_lnx_b[j], rwkv_w_o[j],
                                                    vres, v_first, v_first_c, not last)
        h = h + mod_part(m, 1, 2) * y
        h = h + 0.5 * mod_part(m, 2, 2) * swiglu(modulate(h, norm_w[i, 2], mod_part(m, 2, 0), mod_part(m, 2, 1)), ffn_w_in[i, 1], ffn_w_out[i, 1])
        if not last:
            hc = hc + mod_part(mc, 1, 2) * yc
            hc = hc + 0.5 * mod_part(mc, 2, 2) * swiglu(modulate(hc, norm_w[i, 2], mod_part(mc, 2, 0), mod_part(mc, 2, 1)), ffn_w_in[i, 1], ffn_w_out[i, 1])
    return rms_norm(h, final_norm_w)
```

```python
import contextlib
import numpy as np
import ml_dtypes
import concourse.bass as bass
import concourse.mybir as mybir
from concourse.bass_utils import run_bass_kernel_spmd

F32 = mybir.dt.float32
BF16 = mybir.dt.bfloat16
AF = mybir.ActivationFunctionType
ALU = mybir.AluOpType
AX = mybir.AxisListType

D = 1024
KC = 8
DFF = 2816
RMS_EPS = 1e-6
N_DMA_SEMS = 12


class Buf:
    __slots__ = ("name", "t", "w", "r")

    def __init__(self, name, t):
        self.name = name
        self.t = t
        self.w = {}
        self.r = {}

    def __getitem__(self, idx):
        return self.t[idx]


class Sched:
    def __init__(self, nc, stack):
        self.nc = nc
        self.stack = stack
        self.eng = {"pe": nc.tensor, "dve": nc.vector, "act": nc.scalar,
                    "pool": nc.gpsimd, "sp": nc.sync}
        self.sems = {}
        self.cnt = {}
        for k in self.eng:
            self.sems[k] = stack.enter_context(nc.semaphore("s_" + k))
            self.cnt[k] = 0
        for i in range(N_DMA_SEMS):
            k = "d%d" % i
            self.sems[k] = stack.enter_context(nc.semaphore("s_" + k))
            self.cnt[k] = 0
        self.dma_rr = 0
        self.seen = {e: {} for e in self.eng}
        self.n_inst = 0
        self.n_wait = 0
        self.uid = 0

    def sb(self, name, shape, dt):
        self.uid += 1
        t = self.stack.enter_context(self.nc.sbuf_tensor("%s_%d" % (name, self.uid), list(shape), dt))
        return Buf(name, t)

    def ps(self, name, shape, dt=F32):
        self.uid += 1
        t = self.stack.enter_context(self.nc.psum_tensor("%s_%d" % (name, self.uid), list(shape), dt))
        return Buf(name, t)

    def view(self, name, ap_like=None):
        return Buf(name, ap_like)

    def _wait(self, e, deps):
        seen = self.seen[e]
        for k, v in deps.items():
            if seen.get(k, 0) < v:
                self.eng[e].wait_ge(self.sems[k], v)
                seen[k] = v
                self.n_wait += 1

    def _deps(self, e, reads, writes, pe_accum=False):
        deps = {}

        def add(d, skip=None):
            for k, v in d.items():
                if k == skip:
                    continue
                if deps.get(k, 0) < v:
                    deps[k] = v
        for b in reads:
            add(b.w)
        for b in writes:
            add(b.w, "pe" if (pe_accum and e == "pe") else None)
            add(b.r)
        return deps

    def _mark(self, key, val, reads, writes):
        for b in reads:
            if b.r.get(key, 0) < val:
                b.r[key] = val
        for b in writes:
            b.w[key] = val
            b.r = {}

    def op(self, e, reads, writes, fn, pe_accum=False):
        self._wait(e, self._deps(e, reads, writes, pe_accum))
        inst = fn(self.eng[e])
        self.cnt[e] += 1
        inst.then_inc(self.sems[e], 1)
        self._mark(e, self.cnt[e], reads, writes)
        self.n_inst += 1
        return inst

    def dma(self, q, reads, writes, out, in_, **kw):
        k = "d%d" % self.dma_rr
        self.dma_rr = (self.dma_rr + 1) % N_DMA_SEMS
        deps = self._deps(q, reads, writes)
        if self.cnt[k] > 0 and deps.get(k, 0) < self.cnt[k]:
            deps[k] = self.cnt[k]
        self._wait(q, deps)
        inst = self.eng[q].dma_start(out=out, in_=in_, **kw)
        self.cnt[k] += 16
        inst.then_inc(self.sems[k], 16)
        self._mark(k, self.cnt[k], reads, writes)
        self.n_inst += 1
        return inst

    def wait_all(self, e, bufs):
        deps = {}
        for b in bufs:
            for d in (b.w, b.r):
                for k, v in d.items():
                    if deps.get(k, 0) < v:
                        deps[k] = v
        self._wait(e, deps)


def fm(ap, c0, w):
    return ap.rearrange("(kc p) t -> p kc t", p=128)[:, :, c0:c0 + w]


class WLoader:
    def __init__(self, S, stage_cols=2816, nstage=2):
        self.S = S
        self.stage = [S.sb("wstage%d" % i, [128, stage_cols], F32) for i in range(nstage)]
        self.cols = stage_cols
        self.i = 0
        self.ce = 0

    def load(self, dst, dst_ap_fn, src_rows_ap, n_rowchunks, ncols, col0=0):
        S = self.S
        src = src_rows_ap.rearrange("(rc p) n -> p rc n", p=128)
        for rc in range(n_rowchunks):
            for c0 in range(0, ncols, self.cols):
                c1 = min(ncols, c0 + self.cols)
                st = self.stage[self.i % len(self.stage)]
                self.i += 1
                S.dma("sp", [], [st], st[:, 0:c1 - c0], src[:, rc, col0 + c0:col0 + c1])
                e = ("pool", "act", "pool", "dve")[self.ce % 4]
                self.ce += 1
                o = dst_ap_fn(rc, c0, c1)
                i_ = st[:, 0:c1 - c0]
                if e == "act":
                    S.op("act", [st], [dst], lambda en: en.copy(out=o, in_=i_))
                else:
                    S.op(e, [st], [dst], lambda en: en.tensor_copy(out=o, in_=i_))


def l_tiles(Tl, Tcx, W=512):
    tiles = []
    if Tcx:
        tiles.append((0, Tcx, 1))
    for c in range(0, Tl, W):
        tiles.append((Tcx + c, min(W, Tl - c), 0))
    return tiles


def build_L(passes, Tl, Tcx, NV):
    Tc = Tl + Tcx
    W = 512
    tiles = l_tiles(Tl, Tcx, W)
    nc = bass.Bass("TRN2", target_bir_lowering=False)
    hT = nc.dram_tensor("hT", [D, Tc], F32, kind="ExternalInput").ap()
    vecs_d = nc.dram_tensor("vecs", [128, NV * 16], F32, kind="ExternalInput").ap()
    hout = nc.dram_tensor("hout", [D, Tc], F32, kind="ExternalOutput").ap()
    xnT = nc.dram_tensor("xnT", [D, Tc], BF16, kind="Internal").ap()
    wd = {}
    onT = None
    uT = None
    fin = None
    for p in passes:
        if p[0] == "outproj":
            wd[p[1]] = nc.dram_tensor(p[1], [D, D], F32, kind="ExternalInput").ap()
            onT = nc.dram_tensor("onT", [D, Tc], BF16, kind="ExternalInput").ap()
        elif p[0] == "ffn":
            wd[p[1]] = nc.dram_tensor(p[1], [D, 2 * DFF], F32, kind="ExternalInput").ap()
            wd[p[2]] = nc.dram_tensor(p[2], [DFF, D], F32, kind="ExternalInput").ap()
        elif p[0] == "modout":
            uT = nc.dram_tensor("uT", [D, Tc], BF16, kind="ExternalOutput").ap()
        elif p[0] == "final":
            fin = nc.dram_tensor("fin", [D, Tc], F32, kind="ExternalOutput").ap()

    with contextlib.ExitStack() as st:
        S = Sched(nc, st)
        HJ = 11
        vecs = S.sb("vecs", [128, NV, 2, 8], F32)
        Atile = S.sb("Atile", [128, 2, 8], F32)
        Gtile = S.sb("Gtile", [128, 2, 8], F32)
        ones = S.sb("ones", [128, 128], F32)
        w_in = S.sb("w_in", [128, KC, 2 * HJ * 128], BF16)
        w_out = S.sb("w_out", [128, HJ, D], BF16)
        wl = WLoader(S)
        hts = [S.sb("ht%d" % i, [128, KC, W], F32) for i in range(2)]
        xns = [S.sb("xn%d" % i, [128, KC, W], BF16) for i in range(2)]
        hid = S.sb("hid", [128, HJ, W], BF16)
        sqs = [S.sb("sq%d" % i, [128, W], F32) for i in range(2)]
        tmps = [S.sb("tmp%d" % i, [128, W], F32) for i in range(2)]
        sgs = [S.sb("sg%d" % i, [128, W], F32) for i in range(2)]
        std = S.sb("std", [128, W], F32)
        rstd = S.sb("rstd", [128, W], F32)
        ps_ss = S.ps("ps_ss", [128, W])
        ps_g = [S.ps("ps_g%d" % i, [128, W]) for i in range(2)]
        ps_u = [S.ps("ps_u%d" % i, [128, W]) for i in range(2)]
        ps_y = [S.ps("ps_y%d" % i, [128, W]) for i in range(2)]
        h_in_b = [S.view("hin%d" % i) for i in range(len(tiles))]
        h_out_b = [S.view("hout%d" % i) for i in range(len(tiles))]
        xn_b = [S.view("xnd%d" % i) for i in range(len(tiles))]
        fin_b = S.view("finb")

        S.dma("sp", [], [vecs], vecs[:].rearrange("p a b c -> p (a b c)"), vecs_d)
        S.op("pool", [], [ones], lambda e: e.memset(ones[:], 1.0))

        state = {"src": hT, "srcb": h_in_b, "it": 0}

        def load_h(ti):
            c0, w, ms = tiles[ti]
            ht = hts[state["it"] % 2]
            S.dma("sp", [state["srcb"][ti]], [ht], ht[:, :, 0:w], fm(state["src"], c0, w))
            return ht

        def store_h(ti, ht):
            c0, w, ms = tiles[ti]
            S.dma("pool", [ht], [h_out_b[ti]], fm(hout, c0, w), ht[:, :, 0:w])

        def norm_mod(ht, w, ms, g_idx, xn, shift_idx=None, out_f32=None):
            for kc in range(KC):
                sq = sqs[kc % 2]
                S.op("act", [ht], [sq], lambda e: e.activation(out=sq[:, 0:w], in_=ht[:, kc, 0:w], func=AF.Square))
                S.op("pe", [ones, sq], [ps_ss], lambda e: e.matmul(ps_ss[:, 0:w], ones[:], sq[:, 0:w], start=(kc == 0), stop=(kc == KC - 1)), pe_accum=True)
            S.op("act", [ps_ss], [std], lambda e: e.activation(out=std[:, 0:w], in_=ps_ss[:, 0:w], func=AF.Sqrt, scale=1.0 / D, bias=eps_t[:, 0:1]))
            S.op("dve", [std], [rstd], lambda e: e.reciprocal(out=rstd[:, 0:w], in_=std[:, 0:w]))
            for kc in range(KC):
                tmp = tmps[kc % 2]
                S.op("dve", [ht, rstd], [tmp], lambda e: e.tensor_tensor(out=tmp[:, 0:w], in0=ht[:, kc, 0:w], in1=rstd[:, 0:w], op=ALU.mult))
                o = xn[:, kc, 0:w]
                if shift_idx is None:
                    S.op("act", [tmp, Atile], [xn], lambda e: e.activation(out=o, in_=tmp[:, 0:w], func=AF.Copy, scale=Atile[:, ms, kc:kc + 1]))
                else:
                    S.op("act", [tmp, Atile, vecs], [xn], lambda e: e.activation(out=o, in_=tmp[:, 0:w], func=AF.Identity, scale=Atile[:, ms, kc:kc + 1], bias=vecs[:, shift_idx, ms, kc:kc + 1]))

        eps_t = S.sb("eps_t", [128, 1], F32)
        S.op("pool", [], [eps_t], lambda e: e.memset(eps_t[:], RMS_EPS))

        def prep_A(g_idx, scale_idx):
            for ms in range(2):
                if scale_idx is None:
                    S.op("dve", [vecs], [Atile], lambda e: e.tensor_copy(out=Atile[:, ms, :], in_=vecs[:, g_idx, ms, :]))
                else:
                    S.op("dve", [vecs], [Atile], lambda e: e.scalar_tensor_tensor(out=Atile[:, ms, :], in0=vecs[:, scale_idx, ms, :], scalar=1.0, in1=vecs[:, g_idx, ms, :], op0=ALU.add, op1=ALU.mult))

        def prep_G(gate_idx, f):
            for ms in range(2):
                S.op("dve", [vecs], [Gtile], lambda e: e.tensor_scalar(out=Gtile[:, ms, :], in0=vecs[:, gate_idx, ms, :], scalar1=float(f), scalar2=None, op0=ALU.mult))

        def resid_linear(ht, w, ms, wbuf, nk, rhs_fn, rhs_bufs):
            for oc in range(KC):
                py = ps_y[oc % 2]
                for k in range(nk):
                    S.op("pe", [wbuf] + rhs_bufs, [py], lambda e: e.matmul(py[:, 0:w], wbuf[:, k, oc * 128:(oc + 1) * 128], rhs_fn(k), start=(k == 0), stop=(k == nk - 1)), pe_accum=True)
                S.op("dve", [py, ht, Gtile], [ht], lambda e: e.scalar_tensor_tensor(out=ht[:, oc, 0:w], in0=py[:, 0:w], scalar=Gtile[:, ms, oc:oc + 1], in1=ht[:, oc, 0:w], op0=ALU.mult, op1=ALU.add))

        def ffn_half(xn, ht, w, ms, cnt):
            for j in range(HJ):
                pg = ps_g[j % 2]
                pu = ps_u[j % 2]
                for kc in range(KC):
                    S.op("pe", [w_in, xn], [pg], lambda e: e.matmul(pg[:, 0:w], w_in[:, kc, j * 128:(j + 1) * 128], xn[:, kc, 0:w], start=(kc == 0), stop=(kc == KC - 1)), pe_accum=True)
                for kc in range(KC):
                    S.op("pe", [w_in, xn], [pu], lambda e: e.matmul(pu[:, 0:w], w_in[:, kc, (HJ + j) * 128:(HJ + j + 1) * 128], xn[:, kc, 0:w], start=(kc == 0), stop=(kc == KC - 1)), pe_accum=True)
                sg = sgs[j % 2]
                S.op("act", [pg], [sg], lambda e: e.activation(out=sg[:, 0:w], in_=pg[:, 0:w], func=AF.Silu))
                S.op("dve", [sg, pu], [hid], lambda e: e.tensor_tensor(out=hid[:, j, 0:w], in0=sg[:, 0:w], in1=pu[:, 0:w], op=ALU.mult))
            resid_linear(ht, w, ms, w_out, HJ, lambda k: hid[:, k, 0:w], [hid])

        for p in passes:
            if p[0] == "outproj":
                wl.load(w_in, lambda rc, c0, c1: w_in[:, rc, c0:c1], wd[p[1]], KC, D)
                prep_G(p[2], 1.0)
                for ti, (c0, w, ms) in enumerate(tiles):
                    ht = load_h(ti)
                    xn = xns[state["it"] % 2]
                    S.dma("sp", [], [xn], xn[:, :, 0:w], fm(onT, c0, w))
                    resid_linear(ht, w, ms, w_in, KC, lambda k: xn[:, k, 0:w], [xn])
                    store_h(ti, ht)
                    state["it"] += 1
                state["src"], state["srcb"] = hout, h_out_b
            elif p[0] == "ffn":
                _, wi, wo, g_idx, sh_idx, sc_idx, gt_idx = p
                prep_A(g_idx, sc_idx)
                prep_G(gt_idx, 0.5)
                for half in range(2):
                    wl.load(w_in, lambda rc, c0, c1: w_in[:, rc, c0:c1], wd[wi], KC, HJ * 128, col0=half * HJ * 128)
                    wl.load(w_in, lambda rc, c0, c1: w_in[:, rc, HJ * 128 + c0:HJ * 128 + c1], wd[wi], KC, HJ * 128, col0=DFF + half * HJ * 128)
                    wl.load(w_out, lambda rc, c0, c1: w_out[:, rc, c0:c1], wd[wo][half * HJ * 128:(half + 1) * HJ * 128, :], HJ, D)
                    for ti, (c0, w, ms) in enumerate(tiles):
                        ht = load_h(ti)
                        xn = xns[state["it"] % 2]
                        if half == 0:
                            norm_mod(ht, w, ms, g_idx, xn, sh_idx)
                            S.dma("pool", [xn], [xn_b[ti]], fm(xnT, c0, w), xn[:, :, 0:w])
                        else:
                            S.dma("sp", [xn_b[ti]], [xn], xn[:, :, 0:w], fm(xnT, c0, w))
                        ffn_half(xn, ht, w, ms, state["it"])
                        store_h(ti, ht)
                        state["it"] += 1
                    state["src"], state["srcb"] = hout, h_out_b
            elif p[0] in ("modout", "final"):
                if p[0] == "modout":
                    prep_A(p[1], p[3])
                else:
                    prep_A(p[1], None)
                for ti, (c0, w, ms) in enumerate(tiles):
                    ht = load_h(ti)
                    if p[0] == "modout":
                        xn = xns[state["it"] % 2]
                        norm_mod(ht, w, ms, p[1], xn, p[2])
                        S.dma("pool", [xn], [fin_b], fm(uT, c0, w), xn[:, :, 0:w])
                    else:
                        xo = hts[(state["it"] + 1) % 2]
                        norm_mod(ht, w, ms, p[1], xo, None)
                        S.dma("pool", [xo], [fin_b], fm(fin, c0, w), xo[:, :, 0:w])
                    if state["src"] is hT:
                        store_h(ti, ht)
                    state["it"] += 1
        allb = h_out_b + [fin_b]
        S.wait_all("sp", allb)
        print("L program: inst", S.n_inst, "waits", S.n_wait)
    return nc


def hgrn_consts(C=32, TW=256):
    j = np.arange(C)[:, None]
    i = np.arange(C)[None, :]
    mf = np.tile((j <= i).astype(np.float32), (1, 4))
    mb = np.tile((j >= i).astype(np.float32), (1, 4))
    masks = np.zeros((128, 2 * 4 * C), np.float32)
    masks[:C, :4 * C] = mf
    masks[:C, 4 * C:] = mb
    rmask = np.ones((128, TW), np.float32)
    rmask[:, ::C] = 0.0
    ident = np.eye(128, dtype=np.float32).astype(ml_dtypes.bfloat16)
    return masks, rmask, ident


def build_hgrn(T, Tcx=256, C=32):
    TW = 256
    NCH = TW // C
    NH = 4
    NT = T // TW
    nc = bass.Bass("TRN2", target_bir_lowering=False)
    uT = nc.dram_tensor("uT", [D, T], BF16, kind="ExternalInput").ap()
    w5d = nc.dram_tensor("w5", [D, 5 * 512], F32, kind="ExternalInput").ap()
    lbd = nc.dram_tensor("lbv", [128, 16], F32, kind="ExternalInput").ap()
    jseld = nc.dram_tensor("jsel", [128, 1], F32, kind="ExternalInput").ap()
    ngd = nc.dram_tensor("normg", [1, 512], F32, kind="ExternalInput").ap()
    maskd = nc.dram_tensor("masks", [128, 8 * C], F32, kind="ExternalInput").ap()
    rmaskd = nc.dram_tensor("rmask", [128, TW], F32, kind="ExternalInput").ap()
    identd = nc.dram_tensor("ident", [128, 128], BF16, kind="ExternalInput").ap()
    on = nc.dram_tensor("on", [T, 512], BF16, kind="ExternalOutput").ap()
    ofwd = nc.dram_tensor("ofwd", [T, 512], F32, kind="Internal").ap()

    with contextlib.ExitStack() as st:
        S = Sched(nc, st)
        w5 = S.sb("w5", [128, KC, 5 * 512], BF16)
        wl = WLoader(S, stage_cols=2560)
        lbv = S.sb("lbv", [128, 16], F32)
        jsel = S.sb("jsel", [128, 1], F32)
        lb = S.sb("lb", [128, 8], F32)
        oml = S.sb("oml", [128, 8], F32)
        noml = S.sb("noml", [128, 8], F32)
        normg = S.sb("normg", [C, 512], F32)
        masks = S.sb("masks", [128, 8 * C], F32)
        rmask = S.sb("rmask", [128, TW], F32)
        ident = S.sb("ident", [128, 128], BF16)
        eps_t = S.sb("eps_t", [128, 1], F32)
        ut = S.sb("ut", [128, KC, TW], BF16)
        sig = [S.sb("sig%d" % h, [128, TW], F32) for h in range(NH)]
        key = [S.sb("key%d" % h, [128, TW], F32) for h in range(NH)]
        logf = [S.sb("logf%d" % h, [128, TW], F32) for h in range(NH)]
        qs = [S.sb("qs%d" % h, [128, TW], F32) for h in range(NH)]
        bcum = [S.sb("bcum%d" % h, [128, TW], F32) for h in range(NH)]
        cum = [S.sb("cum%d" % h, [128, TW], F32) for h in range(NH)]
        d3 = [S.sb("d3%d" % h, [128, TW], F32) for h in range(NH)]
        ex = [S.sb("ex%d" % h, [128, TW], F32) for h in range(NH)]
        ebt = [S.sb("ebt%d" % h, [128, NCH], F32) for h in range(NH)]
        qt = [S.sb("qt%d" % h, [128, TW], BF16) for h in range(NH)]
        kh = [S.sb("kh%d" % h, [128, TW], BF16) for h in range(NH)]
        kb = [S.sb("kb%d" % h, [128, TW], BF16) for h in range(NH)]
        Vt = S.sb("Vt", [C, NCH, 512], BF16)
        sgt = S.sb("sgt", [C, NCH, 512], F32)
        obuf = S.sb("obuf", [C, NCH, 512], F32)
        ofl = S.sb("ofl", [C, NCH, 512], F32)
        attT = S.sb("attT", [C, NH * C], BF16)
        kbT = S.sb("kbT", [C, 512], BF16)
        Sst = [S.sb("Sst%d" % h, [128, 128], F32) for h in range(NH)]
        Sbf = S.sb("Sbf", [128, NH, 128], BF16)
        ot = S.sb("ot", [C, 512], F32)
        sq = S.sb("sq", [C, 512], F32)
        ssq = S.sb("ssq", [C, NH], F32)
        rs = S.sb("rs", [C, NH], F32)
        onb = S.sb("onb", [C, NCH, 512], BF16)
        pz = [S.ps("pz%d" % i, [128, 512]) for i in range(2)]
        pv = S.ps("pv", [128, 512])
        patt = S.ps("patt", [128, 512])
        po = S.ps("po", [128, 512])
        ptr = S.ps("ptr", [128, 512], BF16)
        pS = S.ps("pS", [128, 512])
        ofwd_b = [S.view("ofwd%d" % i) for i in range(NT)]
        on_b = S.view("on_b")

        S.dma("sp", [], [lbv], lbv[:], lbd)
        S.dma("sp", [], [jsel], jsel[:], jseld)
        S.dma("sp", [], [normg], normg[:], ngd[0:1, :].to_broadcast([C, 512]))
        S.dma("sp", [], [masks], masks[:], maskd)
        S.dma("sp", [], [rmask], rmask[:], rmaskd)
        S.dma("sp", [], [ident], ident[:], identd)
        S.op("pool", [], [eps_t], lambda e: e.memset(eps_t[:], RMS_EPS))
        S.op("dve", [lbv], [lb], lambda e: e.tensor_tensor(out=lb[:], in0=lbv[:, 8:16], in1=lbv[:, 0:8], op=ALU.subtract))
        S.op("act", [lb], [lb], lambda e: e.activation(out=lb[:], in_=lb[:], func=AF.Sigmoid))
        S.op("dve", [lb, jsel], [lb], lambda e: e.tensor_scalar(out=lb[:], in0=lb[:], scalar1=jsel[:, 0:1], scalar2=None, op0=ALU.mult))
        S.op("dve", [lb], [oml], lambda e: e.tensor_scalar(out=oml[:], in0=lb[:], scalar1=-1.0, scalar2=1.0, op0=ALU.mult, op1=ALU.add))
        S.op("dve", [oml], [noml], lambda e: e.tensor_scalar(out=noml[:], in0=oml[:], scalar1=-1.0, scalar2=None, op0=ALU.mult))
        wl.load(w5, lambda rc, c0, c1: w5[:, rc, c0:c1], w5d, KC, 5 * 512)

        def wcol(grp, h):
            return grp * 512 + h * 128

        for d in range(2):
            for h in range(NH):
                S.op("pool", [], [Sst[h]], lambda e: e.memset(Sst[h][:], 0.0))
            S.op("pool", [], [Sbf], lambda e: e.memset(Sbf[:], 0.0))
            order = list(range(NT)) if d == 0 else [0] + list(range(NT - 1, 0, -1))
            for ti in order:
                t0 = ti * TW
                S.dma("sp", [], [ut], ut[:], fm(uT, t0, TW))
                if d == 1:
                    S.dma("sp", [ofwd_b[ti]], [ofl], ofl[:], ofwd[t0:t0 + TW, :].rearrange("(c i) f -> i c f", i=C))
                for h in range(NH):
                    dh = d * NH + h
                    p1 = pz[0]
                    for kc in range(KC):
                        S.op("pe", [w5, ut], [p1], lambda e: e.matmul(p1[:, 0:TW], w5[:, kc, wcol(1 + d, h):wcol(1 + d, h) + 128], ut[:, kc, :], start=(kc == 0), stop=(kc == KC - 1)), pe_accum=True)
                    S.op("act", [p1], [sig[h]], lambda e: e.activation(out=sig[h][:], in_=p1[:, 0:TW], func=AF.Sigmoid))
                    p2 = pz[1]
                    for kc in range(KC):
                        S.op("pe", [w5, ut], [p2], lambda e: e.matmul(p2[:, 0:TW], w5[:, kc, wcol(3, h):wcol(3, h) + 128], ut[:, kc, :], start=(kc == 0), stop=(kc == KC - 1)), pe_accum=True)
                    S.op("act", [p2], [qs[h]], lambda e: e.activation(out=qs[h][:], in_=p2[:, 0:TW], func=AF.Silu))
                    S.op("dve", [sig[h], oml, noml], [key[h]], lambda e: e.tensor_scalar(out=key[h][:], in0=sig[h][:], scalar1=noml[:, dh:dh + 1], scalar2=oml[:, dh:dh + 1], op0=ALU.mult, op1=ALU.add))
                    S.op("act", [sig[h], oml, lb], [logf[h]], lambda e: e.activation(out=logf[h][:], in_=sig[h][:], func=AF.Ln, scale=oml[:, dh:dh + 1], bias=lb[:, dh:dh + 1]))
                    S.op("dve", [rmask, logf[h]], [bcum[h]], lambda e: e.tensor_tensor_scan(out=bcum[h][:], data0=rmask[:], data1=logf[h][:], initial=0.0, op0=ALU.mult, op1=ALU.add))
                    btot = bcum[h][:, C - 1::C]
                    btot_bc = btot.unsqueeze(2).to_broadcast([128, NCH, C])
                    b3 = bcum[h][:].rearrange("p (c i) -> p c i", i=C)
                    S.op("act", [bcum[h]], [ebt[h]], lambda e: e.activation(out=ebt[h][:], in_=btot, func=AF.Exp))
                    if d == 0:
                        cm = bcum[h]
                        S.op("dve", [bcum[h]], [d3[h]], lambda e: e.tensor_tensor(out=d3[h][:].rearrange("p (c i) -> p c i", i=C), in0=btot_bc, in1=b3, op=ALU.subtract))
                    else:
                        cm = cum[h]
                        S.op("dve", [bcum[h]], [cum[h]], lambda e: e.tensor_tensor(out=cum[h][:].rearrange("p (c i) -> p c i", i=C), in0=btot_bc, in1=b3, op=ALU.subtract))
                        S.op("dve", [cum[h], logf[h]], [cum[h]], lambda e: e.tensor_tensor(out=cum[h][:], in0=cum[h][:], in1=logf[h][:], op=ALU.add))
                        S.op("dve", [bcum[h], logf[h]], [d3[h]], lambda e: e.tensor_tensor(out=d3[h][:], in0=bcum[h][:], in1=logf[h][:], op=ALU.subtract))
                    S.op("act", [cm], [ex[h]], lambda e: e.activation(out=ex[h][:], in_=cm[:], func=AF.Exp))
                    S.op("dve", [qs[h], ex[h]], [qt[h]], lambda e: e.tensor_tensor(out=qt[h][:], in0=qs[h][:], in1=ex[h][:], op=ALU.mult))
                    S.op("act", [cm], [ex[h]], lambda e: e.activation(out=ex[h][:], in_=cm[:], func=AF.Exp, scale=-1.0))
                    S.op("dve", [key[h], ex[h]], [kh[h]], lambda e: e.tensor_tensor(out=kh[h][:], in0=key[h][:], in1=ex[h][:], op=ALU.mult))
                    S.op("act", [d3[h]], [ex[h]], lambda e: e.activation(out=ex[h][:], in_=d3[h][:], func=AF.Exp))
                    S.op("dve", [key[h], ex[h]], [kb[h]], lambda e: e.tensor_tensor(out=kb[h][:], in0=key[h][:], in1=ex[h][:], op=ALU.mult))
                for c in range(NCH):
                    for kc in range(KC):
                        S.op("pe", [w5, ut], [pv], lambda e: e.matmul(pv[0:C, :], ut[:, kc, c * C:(c + 1) * C], w5[:, kc, 0:512], start=(kc == 0), stop=(kc == KC - 1)), pe_accum=True)
                    S.op("act", [pv], [Vt], lambda e: e.copy(out=Vt[:, c, :], in_=pv[0:C, :]))
                    if d == 1:
                        for kc in range(KC):
                            S.op("pe", [w5, ut], [pv], lambda e: e.matmul(pv[0:C, :], ut[:, kc, c * C:(c + 1) * C], w5[:, kc, 4 * 512:5 * 512], start=(kc == 0), stop=(kc == KC - 1)), pe_accum=True)
                        S.op("act", [pv], [sgt], lambda e: e.activation(out=sgt[:, c, :], in_=pv[0:C, :], func=AF.Silu))
                corder = range(NCH) if d == 0 else range(NCH - 1, -1, -1)
                for c in corder:
                    cs = slice(c * C, (c + 1) * C)
                    for h in range(NH):
                        S.op("pe", [kh[h], qt[h]], [patt], lambda e: e.matmul(patt[0:C, h * C:(h + 1) * C], kh[h][:, cs], qt[h][:, cs], start=True, stop=True), pe_accum=True)
                    S.op("dve", [patt, masks], [attT], lambda e: e.tensor_tensor(out=attT[:], in0=patt[0:C, 0:NH * C], in1=masks[0:C, d * NH * C:(d + 1) * NH * C], op=ALU.mult))
                    for h in range(NH):
                        S.op("pe", [qt[h], Sbf], [po], lambda e: e.matmul(po[0:C, h * 128:(h + 1) * 128], qt[h][:, cs], Sbf[:, h, :], start=True, stop=False), pe_accum=True)
                        S.op("pe", [attT, Vt], [po], lambda e: e.matmul(po[0:C, h * 128:(h + 1) * 128], attT[:, h * C:(h + 1) * C], Vt[:, c, h * 128:(h + 1) * 128], start=False, stop=True), pe_accum=True)
                    for h in range(NH):
                        S.op("pe", [kb[h], ident], [ptr], lambda e: e.transpose(ptr[0:C, h * 128:(h + 1) * 128], kb[h][:, cs], ident[:]), pe_accum=True)
                    S.op("act", [ptr], [kbT], lambda e: e.copy(out=kbT[:], in_=ptr[0:C, :]))
                    if d == 0:
                        S.op("dve", [po], [obuf], lambda e: e.tensor_copy(out=obuf[:, c, :], in_=po[0:C, :]))
                    else:
                        S.op("dve", [po, ofl], [ot], lambda e: e.tensor_tensor(out=ot[:], in0=po[0:C, :], in1=ofl[:, c, :], op=ALU.add))
                    for h in range(NH):
                        S.op("pe", [kbT, Vt], [pS], lambda e: e.matmul(pS[:, h * 128:(h + 1) * 128], kbT[:, h * 128:(h + 1) * 128], Vt[:, c, h * 128:(h + 1) * 128], start=True, stop=True), pe_accum=True)
                    for h in range(NH):
                        S.op("dve", [Sst[h], ebt[h], pS], [Sst[h]], lambda e: e.scalar_tensor_tensor(out=Sst[h][:], in0=Sst[h][:], scalar=ebt[h][:, c:c + 1], in1=pS[:, h * 128:(h + 1) * 128], op0=ALU.mult, op1=ALU.add))
                        S.op("act", [Sst[h]], [Sbf], lambda e: e.copy(out=Sbf[:, h, :], in_=Sst[h][:]))
                    if d == 1:
                        S.op("act", [ot], [sq], lambda e: e.activation(out=sq[:], in_=ot[:], func=AF.Square))
                        S.op("dve", [sq], [ssq], lambda e: e.tensor_reduce(out=ssq[:], in_=sq[:].rearrange("p (h v) -> p h v", v=128), axis=AX.X, op=ALU.add))
                        S.op("act", [ssq, eps_t], [rs], lambda e: e.activation(out=rs[:], in_=ssq[:], func=AF.Sqrt, scale=1.0 / 128, bias=eps_t[0:C, 0:1]))
                        S.op("dve", [rs], [rs], lambda e: e.reciprocal(out=rs[:], in_=rs[:]))
                        S.op("dve", [ot, rs], [ot], lambda e: e.tensor_tensor(out=ot[:].rearrange("p (h v) -> p h v", v=128), in0=ot[:].rearrange("p (h v) -> p h v", v=128), in1=rs[:].unsqueeze(2).to_broadcast([C, NH, 128]), op=ALU.mult))
                        S.op("dve", [ot, normg], [ot], lambda e: e.tensor_tensor(out=ot[:], in0=ot[:], in1=normg[:], op=ALU.mult))
                        S.op("dve", [ot, sgt], [onb], lambda e: e.tensor_tensor(out=onb[:, c, :], in0=ot[:], in1=sgt[:, c, :], op=ALU.mult))
                if d == 0:
                    S.dma("pool", [obuf], [ofwd_b[ti]], ofwd[t0:t0 + TW, :].rearrange("(c i) f -> i c f", i=C), obuf[:])
                else:
                    S.dma("pool", [onb], [on_b], on[t0:t0 + TW, :].rearrange("(c i) f -> i c f", i=C), onb[:])
        S.wait_all("sp", [on_b])
        print("HGRN program: inst", S.n_inst, "waits", S.n_wait)
    return nc


GN_EPS = 64e-5
RC = 64
RTW = 128
RNH = 8


def rwkv_consts():
    s = np.arange(64)[:, None]
    t = np.arange(64)[None, :]
    def t8(m):
        return np.tile(m.astype(np.float32), (1, 8))
    cm = np.zeros((64, 5, 512), np.float32)
    cm[:, 0] = t8(s < t)
    cm[:, 1] = t8(s > t)
    cm[:, 2] = t8(s <= t)
    cm[:, 3] = t8(s >= t)
    cm[:, 4] = t8(s == t)
    rm = np.ones((64, RNH * RTW), np.float32)
    rm[:, ::RC] = 0.0
    sm = np.ones((128, 2, RTW), np.float32)
    sm[:, 0, 0::64] = 0.0
    sm[:, 1, 63::64] = 0.0
    ident = np.eye(128, dtype=np.float32).astype(ml_dtypes.bfloat16)
    return cm.reshape(64, 5 * 512), rm, sm.reshape(128, 2 * RTW), ident


def build_rwkv(T, vres, Tcx=256):
    C = RC
    TW = RTW
    NCH = TW // C
    NH = RNH
    NT = T // TW
    NTC = Tcx // TW
    FW = NH * TW
    nc = bass.Bass("TRN2", target_bir_lowering=False)
    dt_in = lambda n, s, d=F32: nc.dram_tensor(n, s, d, kind="ExternalInput").ap()
    uT = dt_in("uT", [D, T], BF16)
    wrkv_d = dt_in("wrkv", [D, 3 * 512])
    wl1_d = dt_in("wl1", [D, 256 + 128 + 32])
    w2_d = dt_in("w2", [64, 2 * 512])
    a2_d = dt_in("a2", [64, 2 * 512])
    g2_d = dt_in("g2", [128, 512])
    v2_d = dt_in("v2", [32, 512])
    mu_d = dt_in("mu", [128, 6 * 8])
    hv_d = dt_in("hvec", [64, 7 * 8])
    rows_d = dt_in("rows", [3, 512])
    cm_d = dt_in("cmask", [64, 5 * 512])
    rm_d = dt_in("rmask", [64, FW])
    sm_d = dt_in("smask", [128, 2 * TW])
    id_d = dt_in("ident", [128, 128], BF16)
    if vres:
        vfirst = dt_in("vfirst", [T, 512])
    else:
        vfirst = nc.dram_tensor("vfirst", [T, 512], F32, kind="ExternalOutput").ap()
    on = nc.dram_tensor("on", [T, 512], BF16, kind="ExternalOutput").ap()
    yfwd = nc.dram_tensor("yfwd", [T, 512], F32, kind="Internal").ap()

    with contextlib.ExitStack() as st:
        S = Sched(nc, st)
        wl = WLoader(S, stage_cols=1536)
        wrkv = S.sb("wrkv", [128, KC, 1536], BF16)
        wl1 = S.sb("wl1", [128, KC, 416], BF16)
        w2 = S.sb("w2", [64, 1024], BF16)
        a2 = S.sb("a2", [64, 1024], BF16)
        g2 = S.sb("g2", [128, 512], BF16)
        v2 = S.sb("v2", [32, 512], BF16)
        mu = S.sb("mu", [128, 6, 8], F32)
        hv = S.sb("hv", [64, 7, 8], F32)
        rows = S.sb("rows", [64, 3, 512], F32)
        cm = S.sb("cm", [64, 5, 512], F32)
        rm = S.sb("rm", [64, FW], F32)
        sm = S.sb("sm", [128, 2, TW], F32)
        ident = S.sb("ident", [128, 128], BF16)
        ones64 = S.sb("ones64", [64, 64], F32)
        onesb = S.sb("onesb", [64, 2], BF16)
        gneps = S.sb("gneps", [64, 1], F32)
        uth = S.sb("uth", [128, KC, TW + 128], BF16)
        xx = S.sb("xx", [128, KC, TW], F32)
        xms = S.sb("xms", [128, TW], F32)
        xl = [S.sb("xl%d" % i, [128, KC, TW], BF16) for i in range(2)]
        Fb = [S.sb("F%d" % i, [64, NH, TW], F32) for i in range(10)]
        kf, rf, kk, lw, bc, av, kn, t1, t2, a0b = Fb
        kkt, bh, khh, rt, Bb, Kb, rkb = [S.sb("o%d" % i, [64, NH, TW], BF16) for i in range(7)]
        ebt = S.sb("ebt", [64, NH, NCH], F32)
        hlo = S.sb("hlo", [128, TW], BF16)
        Vt = S.sb("Vt", [C, NCH, 512], BF16)
        Vf = S.sb("Vf", [C, NCH, 512], F32)
        gt = S.sb("gt", [C, NCH, 512], F32)
        vfl = S.sb("vfl", [C, NCH, 512], F32)
        ybuf = S.sb("ybuf", [C, NCH, 512], F32)
        yfl = S.sb("yfl", [C, NCH, 512], F32)
        onb = S.sb("onb", [C, NCH, 512], BF16)
        Mb = [S.sb("M%d" % i, [C, 512], BF16) for i in range(2)]
        MTb = [S.sb("MT%d" % i, [C, 512], BF16) for i in range(2)]
        Xb = [S.sb("X%d" % i, [C, 512], BF16) for i in range(2)]
        AakT, Zb, Ub, BrbT, BrkT = [S.sb("c%d" % i, [C, 512], BF16) for i in range(5)]
        BKT = S.sb("BKT", [C, 2, 512], BF16)
        yt = S.sb("yt", [C, 512], F32)
        yc = S.sb("yc", [C, 512], F32)
        ysq = S.sb("ysq", [C, 512], F32)
        st8 = [S.sb("st8%d" % i, [C, NH], F32) for i in range(4)]
        Sst = S.sb("Sst", [64, NH, 64], F32)
        Sbf = S.sb("Sbf", [64, NH, 64], BF16)
        pp = [S.ps("pp%d" % i, [128, 512]) for i in range(2)]
        pm = [S.ps("pm%d" % i, [128, 512]) for i in range(4)]
        ptr = S.ps("ptr", [128, 1024], BF16)
        pS = S.ps("pS", [128, 512])
        yfwd_b = [S.view("yfwd%d" % i) for i in range(NT)]
        vf_b = [S.view("vf%d" % i) for i in range(NT)]
        on_b = S.view("on_b")
        pmi = [0]
        ppi = [0]

        def nxt_pm():
            pmi[0] += 1
            return pm[pmi[0] % 4]

        def nxt_pp():
            ppi[0] += 1
            return pp[ppi[0] % 2]

        S.dma("sp", [], [mu], mu[:].rearrange("p a b -> p (a b)"), mu_d)
        S.dma("sp", [], [hv], hv[:].rearrange("p a b -> p (a b)"), hv_d)
        for i in range(3):
            S.dma("sp", [], [rows], rows[:, i, :], rows_d[i:i + 1, :].to_broadcast([64, 512]))
        S.dma("sp", [], [cm], cm[:].rearrange("p a b -> p (a b)"), cm_d)
        S.dma("sp", [], [rm], rm[:], rm_d)
        S.dma("sp", [], [sm], sm[:].rearrange("p a b -> p (a b)"), sm_d)
        S.dma("sp", [], [ident], ident[:], id_d)
        S.op("pool", [], [ones64], lambda e: e.memset(ones64[:], 1.0))
        S.op("pool", [], [onesb], lambda e: e.memset(onesb[:], 1.0))
        S.op("pool", [], [gneps], lambda e: e.memset(gneps[:], GN_EPS))
        wl.load(wrkv, lambda rc, c0, c1: wrkv[:, rc, c0:c1], wrkv_d, KC, 1536)
        wl.load(wl1, lambda rc, c0, c1: wl1[:, rc, c0:c1], wl1_d, KC, 416)
        st0 = wl.stage[0]
        for (dst, src, rws, ncol) in ((w2, w2_d, 64, 1024), (a2, a2_d, 64, 1024), (g2, g2_d, 128, 512), (v2, v2_d, 32, 512)):
            S.dma("sp", [], [st0], st0[0:rws, 0:ncol], src)
            S.op("dve", [st0], [dst], lambda e: e.tensor_copy(out=dst[:], in_=st0[0:rws, 0:ncol]))

        def hb(vec_i):
            return hv[:, vec_i, :].unsqueeze(2).to_broadcast([64, NH, TW])

        def lerp(n, dst):
            for kc in range(KC):
                S.op("dve", [xx, mu, uth], [dst], lambda e: e.scalar_tensor_tensor(out=dst[:, kc, :], in0=xx[:, kc, :], scalar=mu[:, n, kc:kc + 1], in1=uth[:, kc, 64:64 + TW], op0=ALU.mult, op1=ALU.add))

        def proj_fm(dst, xin, wbuf, col0, func=None, bias_vec=None):
            for hp in range(NH // 2):
                p = nxt_pp()
                for hh in range(2):
                    h = hp * 2 + hh
                    for kc in range(KC):
                        S.op("pe", [wbuf, xin], [p], lambda e: e.matmul(p[0:64, hh * TW:(hh + 1) * TW], wbuf[:, kc, col0 + h * 64:col0 + (h + 1) * 64], xin[:, kc, :], start=(kc == 0), stop=(kc == KC - 1)), pe_accum=True)
                S.op("act", [p], [dst], lambda e: e.copy(out=dst[:, hp * 2:hp * 2 + 2, :].rearrange("p a b -> p (a b)"), in_=p[0:64, 0:2 * TW]))

        def lora_fm(dst, xin, col0, nh, w2buf, w2col0, func_h, bias_i, func_o):
            p = nxt_pp()
            for kc in range(KC):
                S.op("pe", [wl1, xin], [p], lambda e: e.matmul(p[0:nh, 0:TW], wl1[:, kc, col0:col0 + nh], xin[:, kc, :], start=(kc == 0), stop=(kc == KC - 1)), pe_accum=True)
            S.op("act", [p], [hlo], lambda e: e.activation(out=hlo[0:nh, :], in_=p[0:nh, 0:TW], func=func_h))
            for hp in range(NH // 2):
                p2 = nxt_pp()
                for hh in range(2):
                    h = hp * 2 + hh
                    S.op("pe", [w2buf, hlo], [p2], lambda e: e.matmul(p2[0:64, hh * TW:(hh + 1) * TW], w2buf[0:nh, w2col0 + h * 64:w2col0 + (h + 1) * 64], hlo[0:nh, :], start=True, stop=True), pe_accum=True)
                for hh in range(2):
                    h = hp * 2 + hh
                    S.op("act", [p2, hv], [dst], lambda e: e.activation(out=dst[:, h, :], in_=p2[0:64, hh * TW:(hh + 1) * TW], func=func_o, bias=hv[:, bias_i, h:h + 1]))

        f2 = lambda b: b[:].rearrange("p a b -> p (a b)")
        c3 = lambda b: b[:].rearrange("p a (c i) -> p a c i", i=C)

        for d in range(2):
            S.op("pool", [], [Sst], lambda e: e.memset(Sst[:], 0.0))
            S.op("pool", [], [Sbf], lambda e: e.memset(Sbf[:], 0.0))
            if d == 0:
                order = list(range(NT))
            else:
                order = list(range(NTC - 1, -1, -1)) + list(range(NT - 1, NTC - 1, -1))
            for ti in order:
                t0 = ti * TW
                is_ctx = ti < NTC
                seg0, seg1 = (0, Tcx) if is_ctx else (Tcx, T)
                lo = max(seg0, t0 - 64)
                hi = min(seg1, t0 + TW + 64)
                if lo > t0 - 64 or hi < t0 + TW + 64:
                    S.op("pool", [], [uth], lambda e: e.memset(uth[:], 0.0))
                S.dma("sp", [], [uth], uth[:, :, lo - (t0 - 64):hi - (t0 - 64)], fm(uT, lo, hi - lo))
                if d == 1:
                    S.dma("sp", [yfwd_b[ti]], [yfl], yfl[:], yfwd[t0:t0 + TW, :].rearrange("(c i) f -> i c f", i=C))
                if vres:
                    S.dma("sp", [], [vfl], vfl[:], vfirst[t0:t0 + TW, :].rearrange("(c i) f -> i c f", i=C))
                ctr = lambda kc: uth[:, kc, 64:64 + TW]
                for kc in range(KC):
                    if is_ctx:
                        off = 63 if kc < 4 else 65
                        msk = None
                    else:
                        off = (63, 63, 65, 65, 0, 0, 128, 128)[kc]
                        msk = (0, 0, 1, 1, None, None, None, None)[kc]
                    src = uth[:, kc, off:off + TW]
                    if msk is None:
                        S.op("dve", [uth], [xx], lambda e: e.tensor_tensor(out=xx[:, kc, :], in0=src, in1=ctr(kc), op=ALU.subtract))
                    else:
                        S.op("dve", [uth, sm], [xms], lambda e: e.tensor_tensor(out=xms[:], in0=src, in1=sm[:, msk, :], op=ALU.mult))
                        S.op("dve", [xms, uth], [xx], lambda e: e.tensor_tensor(out=xx[:, kc, :], in0=xms[:], in1=ctr(kc), op=ALU.subtract))
                xa_ = xl[0]
                lerp(2, xa_)
                proj_fm(kf, xa_, wrkv, 512)
                xb_ = xl[1]
                lerp(0, xb_)
                proj_fm(rf, xb_, wrkv, 0)
                lerp(1, xa_)
                lora_fm(lw, xa_, d * 64, 64, w2, d * 512, AF.Tanh, 0 + d, AF.Sigmoid)
                lerp(4, xb_)
                lora_fm(av, xb_, 128 + d * 64, 64, a2, d * 512, AF.Copy, 2 + d, AF.Sigmoid)
                if d == 1:
                    lora_fm(a0b, xb_, 128, 64, a2, 0, AF.Copy, 2, AF.Sigmoid)
                lerp(3, xa_)
                if vres:
                    p = nxt_pp()
                    for kc in range(KC):
                        S.op("pe", [wl1, xa_], [p], lambda e: e.matmul(p[0:32, 0:TW], wl1[:, kc, 384:416], xa_[:, kc, :], start=(kc == 0), stop=(kc == KC - 1)), pe_accum=True)
                    S.op("act", [p], [hlo], lambda e: e.copy(out=hlo[0:32, :], in_=p[0:32, 0:TW]))
                for c in range(NCH):
                    p = nxt_pp()
                    for kc in range(KC):
                        S.op("pe", [wrkv, xa_], [p], lambda e: e.matmul(p[0:C, :], xa_[:, kc, c * C:(c + 1) * C], wrkv[:, kc, 1024:1536], start=(kc == 0), stop=(kc == KC - 1)), pe_accum=True)
                    if not vres:
                        S.op("act", [p], [Vf], lambda e: e.copy(out=Vf[:, c, :], in_=p[0:C, :]))
                    else:
                        p2 = nxt_pp()
                        S.op("pe", [hlo, v2], [p2], lambda e: e.matmul(p2[0:C, :], hlo[0:32, c * C:(c + 1) * C], v2[:], start=True, stop=True), pe_accum=True)
                        S.op("dve", [p2, rows], [yc], lambda e: e.tensor_tensor(out=yc[:], in0=p2[0:C, :], in1=rows[:, 0, :], op=ALU.add))
                        S.op("act", [yc], [yc], lambda e: e.activation(out=yc[:], in_=yc[:], func=AF.Sigmoid))
                        S.op("dve", [p, vfl], [ysq], lambda e: e.tensor_tensor(out=ysq[:], in0=vfl[:, c, :], in1=p[0:C, :], op=ALU.subtract))
                        S.op("dve", [ysq, yc], [ysq], lambda e: e.tensor_tensor(out=ysq[:], in0=ysq[:], in1=yc[:], op=ALU.mult))
                        S.op("dve", [ysq, p], [Vf], lambda e: e.tensor_tensor(out=Vf[:, c, :], in0=ysq[:], in1=p[0:C, :], op=ALU.add))
                    S.op("act", [Vf], [Vt], lambda e: e.copy(out=Vt[:, c, :], in_=Vf[:, c, :]))
                if d == 0 and not vres:
                    S.dma("pool", [Vf], [vf_b[ti]], vfirst[t0:t0 + TW, :].rearrange("(c i) f -> i c f", i=C), Vf[:])
                if d == 1:
                    lerp(5, xb_)
                    p = nxt_pp()
                    for kc in range(KC):
                        S.op("pe", [wl1, xb_], [p], lambda e: e.matmul(p[:, 0:TW], wl1[:, kc, 256:384], xb_[:, kc, :], start=(kc == 0), stop=(kc == KC - 1)), pe_accum=True)
                    S.op("act", [p], [hlo], lambda e: e.activation(out=hlo[:], in_=p[:, 0:TW], func=AF.Sigmoid))
                    for c in range(NCH):
                        p2 = nxt_pp()
                        S.op("pe", [hlo, g2], [p2], lambda e: e.matmul(p2[0:C, :], hlo[:, c * C:(c + 1) * C], g2[:], start=True, stop=True), pe_accum=True)
                        S.op("act", [p2], [gt], lambda e: e.copy(out=gt[:, c, :], in_=p2[0:C, :]))
                S.op("dve", [kf, hv], [kk], lambda e: e.tensor_tensor(out=kk[:], in0=kf[:], in1=hb(4), op=ALU.mult))
                S.op("act", [kk], [t1], lambda e: e.activation(out=f2(t1), in_=f2(kk), func=AF.Square))
                for q in range(FW // 512):
                    p = nxt_pp()
                    S.op("pe", [ones64, t1], [p], lambda e: e.matmul(p[0:64, :], ones64[:], f2(t1)[:, q * 512:(q + 1) * 512], start=True, stop=True), pe_accum=True)
                    S.op("dve", [p], [t2], lambda e: e.tensor_scalar(out=f2(t2)[:, q * 512:(q + 1) * 512], in0=p[0:64, :], scalar1=1e-24, scalar2=None, op0=ALU.max))
                S.op("act", [t2], [t2], lambda e: e.activation(out=f2(t2), in_=f2(t2), func=AF.Sqrt))
                S.op("dve", [t2], [t2], lambda e: e.reciprocal(out=f2(t2), in_=f2(t2)))
                S.op("dve", [kk, t2], [kk], lambda e: e.tensor_tensor(out=f2(kk), in0=f2(kk), in1=f2(t2), op=ALU.mult))
                S.op("dve", [lw], [lw], lambda e: e.tensor_scalar(out=f2(lw), in0=f2(lw), scalar1=-0.6065306597126334, scalar2=None, op0=ALU.mult))
                S.op("dve", [rm, lw], [bc], lambda e: e.tensor_tensor_scan(out=f2(bc), data0=rm[:], data1=f2(lw), initial=0.0, op0=ALU.mult, op1=ALU.add))
                btot = c3(bc)[:, :, :, C - 1]
                S.op("act", [bc], [ebt], lambda e: e.activation(out=ebt[:], in_=btot, func=AF.Exp))
                S.op("dve", [av, hv], [t1], lambda e: e.scalar_tensor_tensor(out=t1[:], in0=av[:], scalar=-1.0, in1=hb(5), op0=ALU.add, op1=ALU.mult))
                S.op("dve", [t1, kf], [kn], lambda e: e.scalar_tensor_tensor(out=f2(kn), in0=f2(t1), scalar=1.0, in1=f2(kf), op0=ALU.add, op1=ALU.mult))
                if d == 1:
                    S.op("dve", [a0b, av], [t1], lambda e: e.tensor_tensor(out=f2(t1), in0=f2(a0b), in1=f2(av), op=ALU.add))
                    S.op("dve", [t1], [t1], lambda e: e.tensor_scalar(out=f2(t1), in0=f2(t1), scalar1=0.5, scalar2=-1.0, op0=ALU.mult, op1=ALU.add))
                    S.op("dve", [t1, hv], [t1], lambda e: e.tensor_tensor(out=t1[:], in0=t1[:], in1=hb(5), op=ALU.mult))
                    S.op("dve", [t1, kf], [t1], lambda e: e.scalar_tensor_tensor(out=f2(t1), in0=f2(t1), scalar=1.0, in1=f2(kf), op0=ALU.add, op1=ALU.mult))
                    S.op("dve", [t1, rf], [t1], lambda e: e.tensor_tensor(out=f2(t1), in0=f2(t1), in1=f2(rf), op=ALU.mult))
                    S.op("dve", [t1, hv], [rkb], lambda e: e.tensor_tensor(out=rkb[:], in0=t1[:], in1=hb(6), op=ALU.mult))
                S.op("dve", [kk, av], [av], lambda e: e.tensor_tensor(out=f2(av), in0=f2(kk), in1=f2(av), op=ALU.mult))
                btot_bc = btot.unsqueeze(3).to_broadcast([64, NH, NCH, C])
                S.op("dve", [bc], [t2], lambda e: e.tensor_tensor(out=c3(t2), in0=btot_bc, in1=c3(bc), op=ALU.subtract))
                S.op("dve", [bc, lw], [t1], lambda e: e.tensor_tensor(out=f2(t1), in0=f2(bc), in1=f2(lw), op=ALU.subtract))
                if d == 0:
                    cum, excl, rem = bc, t1, t2
                else:
                    S.op("dve", [t2, lw], [bc], lambda e: e.tensor_tensor(out=f2(bc), in0=f2(t2), in1=f2(lw), op=ALU.add))
                    cum, excl, rem = bc, t2, t1
                S.op("act", [excl], [excl], lambda e: e.activation(out=f2(excl), in_=f2(excl), func=AF.Exp))
                S.op("dve", [kk, excl], [kkt], lambda e: e.tensor_tensor(out=f2(kkt), in0=f2(kk), in1=f2(excl), op=ALU.mult))
                S.op("act", [rem], [rem], lambda e: e.activation(out=f2(rem), in_=f2(rem), func=AF.Exp))
                S.op("dve", [av, rem], [Bb], lambda e: e.tensor_tensor(out=f2(Bb), in0=f2(av), in1=f2(rem), op=ALU.mult))
                S.op("pool", [kn, rem], [Kb], lambda e: e.tensor_tensor(out=f2(Kb), in0=f2(kn), in1=f2(rem), op=ALU.mult))
                S.op("act", [cum], [excl], lambda e: e.activation(out=f2(excl), in_=f2(cum), func=AF.Exp))
                S.op("dve", [rf, excl], [rt], lambda e: e.tensor_tensor(out=f2(rt), in0=f2(rf), in1=f2(excl), op=ALU.mult))
                S.op("act", [cum], [excl], lambda e: e.activation(out=f2(excl), in_=f2(cum), func=AF.Exp, scale=-1.0))
                S.op("dve", [av, excl], [bh], lambda e: e.tensor_tensor(out=f2(bh), in0=f2(av), in1=f2(excl), op=ALU.mult))
                S.op("pool", [kn, excl], [khh], lambda e: e.tensor_tensor(out=f2(khh), in0=f2(kn), in1=f2(excl), op=ALU.mult))

                corder = range(NCH) if d == 0 else range(NCH - 1, -1, -1)
                for c in corder:
                    cs = slice(c * C, (c + 1) * C)
                    hs = lambda h: slice(h * 64, (h + 1) * 64)

                    def mm8(p, lfn, rfn, lb, rb, **kw):
                        for h in range(NH):
                            S.op("pe", lb + rb, [p], lambda e: e.matmul(p[0:C, hs(h)], lfn(h), rfn(h), **kw), pe_accum=True)
                    M, MT, X = Mb[0], MTb[0], Xb[0]
                    p = nxt_pm()
                    mm8(p, lambda h: bh[:, h, cs], lambda h: kkt[:, h, cs], [bh], [kkt], start=True, stop=True)
                    S.op("dve", [p, cm], [M], lambda e: e.scalar_tensor_tensor(out=M[:], in0=p[0:C, :], scalar=-1.0, in1=cm[:, d, :], op0=ALU.mult, op1=ALU.mult))
                    p = nxt_pm()
                    mm8(p, lambda h: kkt[:, h, cs], lambda h: bh[:, h, cs], [kkt], [bh], start=True, stop=True)
                    S.op("dve", [p, cm], [MT], lambda e: e.scalar_tensor_tensor(out=MT[:], in0=p[0:C, :], scalar=-1.0, in1=cm[:, 1 - d, :], op0=ALU.mult, op1=ALU.mult))
                    S.op("pool", [M, cm], [X], lambda e: e.tensor_tensor(out=X[:], in0=M[:], in1=cm[:, 4, :], op=ALU.add))
                    for lvl in range(1, 6):
                        Mn, MTn, Xn = Mb[lvl % 2], MTb[lvl % 2], Xb[lvl % 2]
                        p = nxt_pm()
                        mm8(p, lambda h: M[:, hs(h)], lambda h: MT[:, hs(h)], [M], [MT], start=True, stop=True)
                        S.op("act", [p], [MTn], lambda e: e.copy(out=MTn[:], in_=p[0:C, :]))
                        if lvl < 5:
                            p = nxt_pm()
                            mm8(p, lambda h: MT[:, hs(h)], lambda h: M[:, hs(h)], [MT], [M], start=True, stop=True)
                            S.op("dve", [p], [Mn], lambda e: e.tensor_copy(out=Mn[:], in_=p[0:C, :]))
                        p = nxt_pm()
                        mm8(p, lambda h: MTn[:, hs(h)], lambda h: X[:, hs(h)], [MTn], [X], start=True, stop=True)
                        S.op("dve", [p, X], [Xn], lambda e: e.tensor_tensor(out=Xn[:], in0=p[0:C, :], in1=X[:], op=ALU.add))
                        M, MT, X = Mn, MTn, Xn
                    p = nxt_pm()
                    mm8(p, lambda h: khh[:, h, cs], lambda h: kkt[:, h, cs], [khh], [kkt], start=True, stop=True)
                    S.op("dve", [p, cm], [AakT], lambda e: e.tensor_tensor(out=AakT[:], in0=p[0:C, :], in1=cm[:, d, :], op=ALU.mult))
                    p = nxt_pm()
                    for h in range(NH):
                        S.op("pe", [kkt, Sbf], [p], lambda e: e.matmul(p[0:C, hs(h)], kkt[:, h, cs], Sbf[:, h, :], start=True, stop=False), pe_accum=True)
                        S.op("pe", [AakT, Vt], [p], lambda e: e.matmul(p[0:C, hs(h)], AakT[:, hs(h)], Vt[:, c, hs(h)], start=False, stop=True), pe_accum=True)
                    S.op("act", [p], [Zb], lambda e: e.copy(out=Zb[:], in_=p[0:C, :]))
                    p = nxt_pm()
                    mm8(p, lambda h: X[:, hs(h)], lambda h: Zb[:, hs(h)], [X], [Zb], start=True, stop=True)
                    S.op("act", [p], [Ub], lambda e: e.activation(out=Ub[:], in_=p[0:C, :], func=AF.Copy, scale=-1.0))
                    p = nxt_pm()
                    mm8(p, lambda h: bh[:, h, cs], lambda h: rt[:, h, cs], [bh], [rt], start=True, stop=True)
                    S.op("dve", [p, cm], [BrbT], lambda e: e.tensor_tensor(out=BrbT[:], in0=p[0:C, :], in1=cm[:, 2 + d, :], op=ALU.mult))
                    p = nxt_pm()
                    mm8(p, lambda h: khh[:, h, cs], lambda h: rt[:, h, cs], [khh], [rt], start=True, stop=True)
                    S.op("dve", [p, cm], [BrkT], lambda e: e.tensor_tensor(out=BrkT[:], in0=p[0:C, :], in1=cm[:, 2 + d, :], op=ALU.mult))
                    p = nxt_pm()
                    for h in range(NH):
                        S.op("pe", [rt, Sbf], [p], lambda e: e.matmul(p[0:C, hs(h)], rt[:, h, cs], Sbf[:, h, :], start=True, stop=False), pe_accum=True)
                        S.op("pe", [BrbT, Ub], [p], lambda e: e.matmul(p[0:C, hs(h)], BrbT[:, hs(h)], Ub[:, hs(h)], start=False, stop=False), pe_accum=True)
                        S.op("pe", [BrkT, Vt], [p], lambda e: e.matmul(p[0:C, hs(h)], BrkT[:, hs(h)], Vt[:, c, hs(h)], start=False, stop=True), pe_accum=True)
                    if d == 0:
                        S.op("act", [p], [ybuf], lambda e: e.copy(out=ybuf[:, c, :], in_=p[0:C, :]))
                    else:
                        S.op("dve", [p, yfl], [yt], lambda e: e.tensor_tensor(out=yt[:], in0=p[0:C, :], in1=yfl[:, c, :], op=ALU.add))
                    for h in range(NH):
                        S.op("pe", [Bb, ident], [ptr], lambda e: e.transpose(ptr[0:C, h * 64:(h + 1) * 64], Bb[:, h, cs], ident[0:64, 0:64]), pe_accum=True)
                        S.op("pe", [Kb, ident], [ptr], lambda e: e.transpose(ptr[0:C, 512 + h * 64:512 + (h + 1) * 64], Kb[:, h, cs], ident[0:64, 0:64]), pe_accum=True)
                    S.op("act", [ptr], [BKT], lambda e: e.copy(out=BKT[:].rearrange("p a b -> p (a b)"), in_=ptr[0:C, :]))
                    for h in range(NH):
                        S.op("pe", [BKT, Ub], [pS], lambda e: e.matmul(pS[0:64, hs(h)], BKT[:, 0, hs(h)], Ub[:, hs(h)], start=True, stop=False), pe_accum=True)
                        S.op("pe", [BKT, Vt], [pS], lambda e: e.matmul(pS[0:64, hs(h)], BKT[:, 1, hs(h)], Vt[:, c, hs(h)], start=False, stop=True), pe_accum=True)
                    S.op("dve", [Sst, ebt], [Sst], lambda e: e.tensor_tensor(out=Sst[:], in0=Sst[:], in1=ebt[:, :, c:c + 1].to_broadcast([64, NH, 64]), op=ALU.mult))
                    S.op("dve", [Sst, pS], [Sst], lambda e: e.tensor_tensor(out=f2(Sst), in0=f2(Sst), in1=pS[0:64, :], op=ALU.add))
                    S.op("act", [Sst], [Sbf], lambda e: e.copy(out=f2(Sbf), in_=f2(Sst)))
                    if d == 1:
                        y3 = lambda b: b[:].rearrange("p (h v) -> p h v", v=64)
                        bc8 = lambda b: b[:].unsqueeze(2).to_broadcast([C, NH, 64])
                        S.op("dve", [yt], [st8[0]], lambda e: e.tensor_reduce(out=st8[0][:], in_=y3(yt), axis=AX.X, op=ALU.add))
                        S.op("dve", [st8[0]], [st8[0]], lambda e: e.tensor_scalar(out=st8[0][:], in0=st8[0][:], scalar1=1.0 / 64, scalar2=None, op0=ALU.mult))
                        S.op("dve", [yt, st8[0]], [yc], lambda e: e.tensor_tensor(out=y3(yc), in0=y3(yt), in1=bc8(st8[0]), op=ALU.subtract))
                        S.op("act", [yc], [ysq], lambda e: e.activation(out=ysq[:], in_=yc[:], func=AF.Square))
                        S.op("dve", [ysq], [st8[1]], lambda e: e.tensor_reduce(out=st8[1][:], in_=y3(ysq), axis=AX.X, op=ALU.add))
                        S.op("act", [st8[1], gneps], [st8[1]], lambda e: e.activation(out=st8[1][:], in_=st8[1][:], func=AF.Sqrt, scale=1.0 / 64, bias=gneps[:, 0:1]))
                        S.op("dve", [st8[1]], [st8[1]], lambda e: e.reciprocal(out=st8[1][:], in_=st8[1][:]))
                        S.op("dve", [yc, st8[1]], [yc], lambda e: e.tensor_tensor(out=y3(yc), in0=y3(yc), in1=bc8(st8[1]), op=ALU.mult))
                        S.op("dve", [yc, rows], [yc], lambda e: e.tensor_tensor(out=yc[:], in0=yc[:], in1=rows[:, 1, :], op=ALU.mult))
                        S.op("dve", [yc, rows], [yc], lambda e: e.tensor_tensor(out=yc[:], in0=yc[:], in1=rows[:, 2, :], op=ALU.add))
                        p = nxt_pm()
                        for h in range(NH):
                            S.op("pe", [rkb, onesb], [p], lambda e: e.matmul(p[0:C, 2 * h:2 * h + 2], rkb[:, h, cs], onesb[:], start=True, stop=True), pe_accum=True)
                        S.op("act", [p], [st8[2]], lambda e: e.copy(out=st8[2][:], in_=p[0:C, 0:2 * NH:2]))
                        S.op("dve", [Vf, st8[2]], [ysq], lambda e: e.tensor_tensor(out=y3(ysq), in0=Vf[:, c, :].rearrange("p (h v) -> p h v", v=64), in1=bc8(st8[2]), op=ALU.mult))
                        S.op("dve", [yc, ysq], [yc], lambda e: e.tensor_tensor(out=yc[:], in0=yc[:], in1=ysq[:], op=ALU.add))
                        S.op("dve", [yc, gt], [onb], lambda e: e.tensor_tensor(out=onb[:, c, :], in0=yc[:], in1=gt[:, c, :], op=ALU.mult))
                if d == 0:
                    S.dma("pool", [ybuf], [yfwd_b[ti]], yfwd[t0:t0 + TW, :].rearrange("(c i) f -> i c f", i=C), ybuf[:])
                else:
                    S.dma("pool", [onb], [on_b], on[t0:t0 + TW, :].rearrange("(c i) f -> i c f", i=C), onb[:])
        S.wait_all("sp", [on_b] + vf_b)
        print("RWKV program: inst", S.n_inst, "waits", S.n_wait)
    return nc


MODC = 1152


def build_mod(depth=4):
    nc = bass.Bass("TRN2", target_bir_lowering=False)
    wd = nc.dram_tensor("modw", [depth, D, MODC], F32, kind="ExternalInput").ap()
    bd = nc.dram_tensor("modb", [128, depth * 9], F32, kind="ExternalInput").ap()
    cd = nc.dram_tensor("cT", [128, KC * 8], F32, kind="ExternalInput").ap()
    md = nc.dram_tensor("m", [128, depth * 9 * 8], F32, kind="ExternalOutput").ap()
    with contextlib.ExitStack() as st:
        S = Sched(nc, st)
        W = [S.sb("W%d" % i, [128, KC, MODC], F32) for i in range(2)]
        bt = S.sb("bt", [128, depth, 9], F32)
        ct = S.sb("ct", [128, KC, 8], F32)
        sc = S.sb("sc", [128, KC, 8], F32)
        mo = S.sb("mo", [128, depth, 9, 8], F32)
        ps = [S.ps("ps%d" % i, [128, 512]) for i in range(2)]
        S.dma("sp", [], [bt], bt[:].rearrange("p a b -> p (a b)"), bd)
        S.dma("sp", [], [ct], ct[:].rearrange("p a b -> p (a b)"), cd)
        S.op("act", [ct], [sc], lambda e: e.activation(out=sc[:].rearrange("p a b -> p (a b)"), in_=ct[:].rearrange("p a b -> p (a b)"), func=AF.Silu))
        for l in range(depth):
            Wl = W[l % 2]
            for kc in range(KC):
                S.dma("sp", [], [Wl], Wl[:, kc, :], wd[l, kc * 128:(kc + 1) * 128, :])
            for j in range(9):
                p = ps[j % 2]
                for kc in range(KC):
                    S.op("pe", [Wl, sc], [p], lambda e: e.matmul(p[:, 0:8], Wl[:, kc, j * 128:(j + 1) * 128], sc[:, kc, :], start=(kc == 0), stop=(kc == KC - 1)), pe_accum=True)
                S.op("dve", [p, bt], [mo], lambda e: e.tensor_scalar(out=mo[:, l, j, :], in0=p[:, 0:8], scalar1=bt[:, l, j:j + 1], scalar2=None, op0=ALU.add))
        out_b = S.view("out")
        S.dma("sp", [mo], [out_b], md, mo[:].rearrange("p a b c -> p (a b c)"))
        S.wait_all("sp", [out_b])
    return nc


BATCH = 4
SEQ = 8192
CTX = 256
DEPTH = 4
NCORE = 8
_BF = ml_dtypes.bfloat16


def _run(nc, ims):
    return run_bass_kernel_spmd(nc, ims, core_ids=list(range(NCORE))).results


def _pack_vecs(vl):
    a = np.stack([np.stack([l, c], 0) for (l, c) in vl], 0)
    nv = a.shape[0]
    return np.ascontiguousarray(a.reshape(nv, 2, 8, 128).transpose(3, 0, 1, 2)).reshape(128, nv * 16)


def _hv(v, hg):
    return v[hg * 512:(hg + 1) * 512].reshape(8, 64).T


def kernel(x, c, ctx, c_ctx, norm_w, mod_w, mod_b, ffn_w_in, ffn_w_out,
           hgrn_w_in, hgrn_lb, hgrn_norm_w, hgrn_w_out,
           rwkv_mu, rwkv_w_rkv, rwkv_w_o, rwkv_w0, rwkv_w1, rwkv_w2,
           rwkv_a0, rwkv_a1, rwkv_a2, rwkv_v0, rwkv_v1, rwkv_v2,
           rwkv_g1, rwkv_g2, rwkv_k_k, rwkv_k_a, rwkv_r_k, rwkv_lnx_w, rwkv_lnx_b,
           final_norm_w):
    f32 = lambda a: np.asarray(a, dtype=np.float32)
    x, c, ctx, c_ctx, norm_w, mod_w, mod_b = map(f32, (x, c, ctx, c_ctx, norm_w, mod_w, mod_b))
    ffn_w_in, ffn_w_out, hgrn_w_in, hgrn_lb, hgrn_norm_w, hgrn_w_out = map(f32, (ffn_w_in, ffn_w_out, hgrn_w_in, hgrn_lb, hgrn_norm_w, hgrn_w_out))
    Tl = SEQ // 2
    Tcx = CTX // 2
    Tc = Tl + Tcx
    T = SEQ + CTX

    ncm = build_mod(DEPTH)
    cT = np.zeros((D, 8), np.float32)
    cT[:, 0:4] = c.T
    cT[:, 4] = c_ctx
    cTp = np.ascontiguousarray(cT.reshape(8, 128, 8).transpose(1, 0, 2)).reshape(128, 64)
    ims = []
    for i in range(NCORE):
        cs = slice(i * MODC, (i + 1) * MODC)
        ims.append({"modw": np.ascontiguousarray(mod_w[:, :, cs]),
                    "modb": np.ascontiguousarray(mod_b[:, cs].reshape(DEPTH, 9, 128).transpose(2, 0, 1)).reshape(128, DEPTH * 9),
                    "cT": cTp})
    res = _run(ncm, ims)
    m = np.zeros((DEPTH, 9 * D, 8), np.float32)
    for i in range(NCORE):
        mi = res[i]["m"].reshape(128, DEPTH, 9, 8)
        m[:, i * MODC:(i + 1) * MODC, :] = mi.transpose(1, 2, 0, 3).reshape(DEPTH, MODC, 8)

    def mvec(l, sub, k, b):
        s = (3 * sub + k) * D
        return (m[l, s:s + D, b], m[l, s:s + D, 4])

    def gvec(g):
        return (g, g)

    hT = []
    for core in range(NCORE):
        b, half = core // 2, core % 2
        tok = np.concatenate([ctx[b, half * Tcx:(half + 1) * Tcx], x[b, half * Tl:(half + 1) * Tl]], 0)
        hT.append(np.ascontiguousarray(tok.T))

    def ffn_vecs(l, sub, nw_i, b):
        return [gvec(norm_w[l, nw_i]), mvec(l, sub, 0, b), mvec(l, sub, 1, b), mvec(l, sub, 2, b)]

    progs = {}

    def get_L(key, passes, nv):
        if key not in progs:
            progs[key] = build_L(passes, Tl, Tcx, nv)
        return progs[key]

    hmasks, hrmask, hident = hgrn_consts()
    rcm, rrm, rsm, rident = rwkv_consts()
    vfirst = [None] * NCORE
    onT = None
    out = None
    for i in range(DEPTH + 1):
        passes = []
        nv = 0
        wnames = {}
        if i > 0:
            passes.append(("outproj", "w_o", nv)); nv += 1
            passes.append(("ffn", "w_inA", "w_outA", nv, nv + 1, nv + 2, nv + 3)); nv += 4
        if i < DEPTH:
            passes.append(("ffn", "w_inB", "w_outB", nv, nv + 1, nv + 2, nv + 3)); nv += 4
            passes.append(("modout", nv, nv + 1, nv + 2)); nv += 3
        else:
            passes.append(("final", nv)); nv += 1
        ncl = get_L("L%d" % (0 if i == 0 else (2 if i == DEPTH else 1)), passes, nv)
        ims = []
        for core in range(NCORE):
            b = core // 2
            vl = []
            im = {"hT": hT[core]}
            if i > 0:
                l = i - 1
                vl.append(mvec(l, 1, 2, b))
                vl += ffn_vecs(l, 2, 2, b)
                im["w_o"] = hgrn_w_out[l // 2] if l % 2 == 0 else f32(rwkv_w_o[l // 2])
                im["w_inA"] = ffn_w_in[l, 1]
                im["w_outA"] = ffn_w_out[l, 1]
                im["onT"] = onT[core]
            if i < DEPTH:
                vl += ffn_vecs(i, 0, 0, b)
                vl += [gvec(norm_w[i, 1]), mvec(i, 1, 0, b), mvec(i, 1, 1, b)]
                im["w_inB"] = ffn_w_in[i, 0]
                im["w_outB"] = ffn_w_out[i, 0]
            else:
                vl.append(gvec(f32(final_norm_w)))
            im["vecs"] = _pack_vecs(vl)
            ims.append(im)
        res = _run(ncl, ims)
        if i == DEPTH:
            out = np.zeros((BATCH, SEQ, D), np.float32)
            for core in range(NCORE):
                b, half = core // 2, core % 2
                out[b, half * Tl:(half + 1) * Tl, :] = res[core]["fin"][:, Tcx:].T
            break
        hT = [res[core]["hout"] for core in range(NCORE)]
        j = i // 2
        ims = []
        for core in range(NCORE):
            b, hg = core // 2, core % 2
            u0, u1 = res[2 * b]["uT"], res[2 * b + 1]["uT"]
            ucat = np.ascontiguousarray(np.concatenate([u0[:, :Tcx], u1[:, :Tcx], u0[:, Tcx:], u1[:, Tcx:]], 1))
            cs = slice(hg * 512, (hg + 1) * 512)
            if i % 2 == 0:
                cols = np.concatenate([np.arange(g * D + hg * 512, g * D + hg * 512 + 512) for g in range(5)])
                lbv = hgrn_lb[:, :, cs].reshape(2, 2, 4, 128).transpose(3, 0, 1, 2).reshape(128, 16)
                ims.append({"uT": ucat, "w5": np.ascontiguousarray(hgrn_w_in[j][:, cols]), "lbv": np.ascontiguousarray(lbv),
                            "jsel": np.full((128, 1), float(j), np.float32), "normg": np.ascontiguousarray(hgrn_norm_w[j][None, cs]),
                            "masks": hmasks, "rmask": hrmask, "ident": hident})
            else:
                wr = f32(rwkv_w_rkv[j])
                w1, w2, a1, a2 = f32(rwkv_w1[j]), f32(rwkv_w2[j]), f32(rwkv_a1[j]), f32(rwkv_a2[j])
                vj = max(j - 1, 0)
                v1 = f32(rwkv_v1[vj]) if j > 0 else np.zeros((D, 32), np.float32)
                v2 = f32(rwkv_v2[vj]) if j > 0 else np.zeros((32, D), np.float32)
                v0 = f32(rwkv_v0[vj]) if j > 0 else np.zeros((D,), np.float32)
                im = {"uT": ucat,
                      "wrkv": np.ascontiguousarray(np.concatenate([wr[0][:, cs], wr[1][:, cs], wr[2][:, cs]], 1)),
                      "wl1": np.ascontiguousarray(np.concatenate([w1[0], w1[1], a1[0], a1[1], f32(rwkv_g1[j]), v1], 1)),
                      "w2": np.ascontiguousarray(np.concatenate([w2[0][:, cs], w2[1][:, cs]], 1)),
                      "a2": np.ascontiguousarray(np.concatenate([a2[0][:, cs], a2[1][:, cs]], 1)),
                      "g2": np.ascontiguousarray(f32(rwkv_g2[j])[:, cs]), "v2": np.ascontiguousarray(v2[:, cs]),
                      "mu": np.ascontiguousarray(f32(rwkv_mu[j]).reshape(6, 8, 128).transpose(2, 0, 1)).reshape(128, 48),
                      "hvec": np.ascontiguousarray(np.stack([_hv(f32(rwkv_w0[j])[0], hg), _hv(f32(rwkv_w0[j])[1], hg),
                                                             _hv(f32(rwkv_a0[j])[0], hg), _hv(f32(rwkv_a0[j])[1], hg),
                                                             _hv(f32(rwkv_k_k[j]), hg), _hv(f32(rwkv_k_a[j]), hg),
                                                             _hv(f32(rwkv_r_k[j]).reshape(-1), hg)], 1)).reshape(64, 56),
                      "rows": np.ascontiguousarray(np.stack([v0[cs], f32(rwkv_lnx_w[j])[cs], f32(rwkv_lnx_b[j])[cs]], 0)),
                      "cmask": rcm, "rmask": rrm, "smask": rsm, "ident": rident}
                if j > 0:
                    im["vfirst"] = vfirst[core]
                ims.append(im)
        if i % 2 == 0:
            if "H" not in progs:
                progs["H"] = build_hgrn(T)
            mres = _run(progs["H"], ims)
        else:
            key = "R%d" % (1 if j > 0 else 0)
            if key not in progs:
                progs[key] = build_rwkv(T, j > 0)
            mres = _run(progs[key], ims)
            if j == 0:
                vfirst = [mres[core]["vfirst"] for core in range(NCORE)]
        onT = []
        for core in range(NCORE):
            b, half = core // 2, core % 2
            o = np.concatenate([mres[2 * b]["on"], mres[2 * b + 1]["on"]], 1)
            tok = np.concatenate([o[half * Tcx:(half + 1) * Tcx], o[CTX + half * Tl:CTX + (half + 1) * Tl]], 0)
            onT.append(np.ascontiguousarray(tok.T))
    return out
```
